# Optimizing a Trainium2 kernel written in Bass

```python
import jax, jax.numpy as jnp
from jax import lax
import numpy as np

D_MODEL = 1024
BATCH = 1
SEQ = 16384
DEPTH = 2

N_MIXERS = 2
ROPE_THETA = 500000.0
Q_BLOCK = 128
NEG_INF = -1e30
LN_EPS = 1e-5
RMS_EPS = 1e-6
MLA_HEADS = 8
MLA_Q_RANK = 384
MLA_KV_RANK = 256
MLA_NOPE_DIM = 128
MLA_ROPE_DIM = 64
MLA_V_DIM = 128
MLA_QK_DIM = MLA_NOPE_DIM + MLA_ROPE_DIM
MOBA_HEADS = 8
MOBA_HEAD_DIM = D_MODEL // MOBA_HEADS
MOBA_ROT_DIM = MOBA_HEAD_DIM // 4
MOBA_BLOCK = 256
MOBA_TOPK = 3
MOBA_Q_CHUNK = 32
D_FF = -(-(8 * D_MODEL) // (3 * 256)) * 256
DEEPNORM_ALPHA = (2 * DEPTH) ** 0.25
DEEPNORM_BETA = (8 * DEPTH) ** -0.25
N_MLA_LAYERS = (DEPTH + 1) // 2
N_MOBA_LAYERS = DEPTH // 2

kernel_name = "hybrid_mla_moba_deepnorm"


def layer_norm(x, g, b):
    xf = x.astype(jnp.float32)
    mu = xf.mean(-1, keepdims=True)
    var = jnp.square(xf - mu).mean(-1, keepdims=True)
    return ((xf - mu) * lax.rsqrt(var + LN_EPS) * g + b).astype(x.dtype)


def rms_norm(x, g):
    xf = x.astype(jnp.float32)
    return (xf * lax.rsqrt(jnp.square(xf).mean(-1, keepdims=True) + RMS_EPS) * g).astype(x.dtype)


def rotary_tables(seq_len, rot_dim):
    inv_freq = ROPE_THETA ** (-jnp.arange(0, rot_dim, 2, dtype=jnp.float32) / rot_dim)
    ang = jnp.arange(seq_len, dtype=jnp.float32)[:, None] * inv_freq[None, :]
    return jnp.cos(ang), jnp.sin(ang)


def apply_rope(x, cos, sin):
    x1, x2 = jnp.split(x, 2, axis=-1)
    out = jnp.concatenate([x1 * cos - x2 * sin, x2 * cos + x1 * sin], axis=-1)
    return out.astype(x.dtype)


def partial_rope(x, cos, sin):
    return jnp.concatenate([apply_rope(x[..., :MOBA_ROT_DIM], cos, sin), x[..., MOBA_ROT_DIM:]], axis=-1)


def mla_mixer(x, cos, sin, w_dqkv, q_norm, w_uq, kv_norm, w_ukv, w_o):
    B, S, _ = x.shape
    H = MLA_HEADS
    lat = x @ w_dqkv
    c_q, c_kv, k_rope = jnp.split(lat, [MLA_Q_RANK, MLA_Q_RANK + MLA_KV_RANK], axis=-1)
    q = (rms_norm(c_q, q_norm) @ w_uq).reshape(B, S, H, MLA_QK_DIM).transpose(0, 2, 1, 3)
    q_nope = q[..., :MLA_NOPE_DIM]
    q_rope = apply_rope(q[..., MLA_NOPE_DIM:], cos, sin)
    k_rope = apply_rope(k_rope, cos, sin)
    kv = (rms_norm(c_kv, kv_norm) @ w_ukv).reshape(B, S, H, MLA_NOPE_DIM + MLA_V_DIM).transpose(0, 2, 1, 3)
    k_nope, v = kv[..., :MLA_NOPE_DIM], kv[..., MLA_NOPE_DIM:]
    scale = MLA_QK_DIM ** -0.5
    kpos = jnp.arange(S)

    def attend_block(i):
        start = i * Q_BLOCK
        qn = lax.dynamic_slice_in_dim(q_nope, start, Q_BLOCK, axis=2)
        qr = lax.dynamic_slice_in_dim(q_rope, start, Q_BLOCK, axis=2)
        s = (jnp.einsum('bhqd,bhkd->bhqk', qn, k_nope)
             + jnp.einsum('bhqr,bkr->bhqk', qr, k_rope)).astype(jnp.float32) * scale
        qpos = start + jnp.arange(Q_BLOCK)
        s = jnp.where(kpos[None, :] <= qpos[:, None], s, NEG_INF)
        p = jax.nn.softmax(s, axis=-1).astype(v.dtype)
        return jnp.einsum('bhqk,bhkv->bhqv', p, v)

    out = lax.map(attend_block, jnp.arange(S // Q_BLOCK))
    out = out.transpose(1, 0, 3, 2, 4).reshape(B, S, H * MLA_V_DIM)
    return out @ w_o


def moba_mixer(x, cos, sin, w_qkv, w_o):
    B, S, _ = x.shape
    H, Dh = MOBA_HEADS, MOBA_HEAD_DIM
    qkv = (x @ w_qkv).reshape(B, S, 3, H, Dh).transpose(2, 0, 3, 1, 4)
    q = partial_rope(qkv[0], cos, sin)
    k = partial_rope(qkv[1], cos, sin)
    v = qkv[2]
    nb = max(-(-S // MOBA_BLOCK), MOBA_TOPK)
    pad = nb * MOBA_BLOCK - S
    k_blocks = jnp.pad(k, ((0, 0), (0, 0), (0, pad), (0, 0))).reshape(B, H, nb, MOBA_BLOCK, Dh)
    v_blocks = jnp.pad(v, ((0, 0), (0, 0), (0, pad), (0, 0))).reshape(B, H, nb, MOBA_BLOCK, Dh)
    k_mean = k_blocks.astype(jnp.float32).mean(axis=3).astype(k.dtype)
    scale = Dh ** -0.5
    bi = jnp.arange(B)[:, None, None, None]
    hi = jnp.arange(H)[None, :, None, None]
    blk_ids = jnp.arange(nb)
    offs = jnp.arange(MOBA_BLOCK)
    n_sel = MOBA_TOPK * MOBA_BLOCK

    def attend_chunk(i):
        start = i * MOBA_Q_CHUNK
        qc = lax.dynamic_slice_in_dim(q, start, MOBA_Q_CHUNK, axis=2)
        qpos = start + jnp.arange(MOBA_Q_CHUNK)
        own = start // MOBA_BLOCK
        gate = jnp.einsum('bhqd,bhnd->bhqn', qc, k_mean).astype(jnp.float32)
        gate = jnp.where(blk_ids < own, gate, NEG_INF)
        _, sel = lax.top_k(gate, MOBA_TOPK)
        sel_valid = sel < own
        k_sel = k_blocks[bi, hi, sel]
        v_sel = v_blocks[bi, hi, sel]
        s_sel = jnp.einsum('bhqd,bhqnkd->bhqnk', qc, k_sel).astype(jnp.float32) * scale
        s_sel = jnp.where(sel_valid[..., None], s_sel, NEG_INF).reshape(B, H, MOBA_Q_CHUNK, n_sel)
        k_own = lax.dynamic_index_in_dim(k_blocks, own, axis=2, keepdims=False)
        v_own = lax.dynamic_index_in_dim(v_blocks, own, axis=2, keepdims=False)
        s_own = jnp.einsum('bhqd,bhkd->bhqk', qc, k_own).astype(jnp.float32) * scale
        kpos = own * MOBA_BLOCK + offs
        s_own = jnp.where(kpos[None, :] <= qpos[:, None], s_own, NEG_INF)
        p = jax.nn.softmax(jnp.concatenate([s_sel, s_own], axis=-1), axis=-1).astype(v.dtype)
        p_sel = p[..., :n_sel].reshape(B, H, MOBA_Q_CHUNK, MOBA_TOPK, MOBA_BLOCK)
        p_own = p[..., n_sel:]
        return (jnp.einsum('bhqnk,bhqnkd->bhqd', p_sel, v_sel)
                + jnp.einsum('bhqk,bhkd->bhqd', p_own, v_own))

    out = lax.map(attend_chunk, jnp.arange(S // MOBA_Q_CHUNK))
    out = out.transpose(1, 0, 3, 2, 4).reshape(B, S, H * Dh)
    return out @ w_o


def swiglu(x, w_in, w_out):
    g, u = jnp.split(x @ w_in, 2, axis=-1)
    return (jax.nn.silu(g) * u) @ w_out


def setup_inputs(seed: int = 0) -> dict:
    key = jax.random.key(seed)
    ks = jax.random.split(key, 20)
    nrm = lambda k, shape, fan_in, gain=1.0: jax.random.normal(k, shape, jnp.float32) * (fan_in ** -0.5) * gain
    gain_vec = lambda k, shape: 1.0 + 0.02 * jax.random.normal(k, shape, jnp.float32)
    Lm, Lb, L = N_MLA_LAYERS, N_MOBA_LAYERS, DEPTH
    return {
        "x": jax.random.normal(ks[0], (BATCH, SEQ, D_MODEL), jnp.float32),
        "mla_w_dqkv": nrm(ks[1], (Lm, D_MODEL, MLA_Q_RANK + MLA_KV_RANK + MLA_ROPE_DIM), D_MODEL),
        "mla_q_norm": gain_vec(ks[2], (Lm, MLA_Q_RANK)),
        "mla_w_uq": nrm(ks[3], (Lm, MLA_Q_RANK, MLA_HEADS * MLA_QK_DIM), MLA_Q_RANK),
        "mla_kv_norm": gain_vec(ks[4], (Lm, MLA_KV_RANK)),
        "mla_w_ukv": nrm(ks[5], (Lm, MLA_KV_RANK, MLA_HEADS * (MLA_NOPE_DIM + MLA_V_DIM)), MLA_KV_RANK),
        "mla_w_o": nrm(ks[6], (Lm, MLA_HEADS * MLA_V_DIM, D_MODEL), MLA_HEADS * MLA_V_DIM, DEEPNORM_BETA),
        "moba_w_qkv": nrm(ks[7], (Lb, D_MODEL, 3 * MOBA_HEADS * MOBA_HEAD_DIM), D_MODEL),
        "moba_w_o": nrm(ks[8], (Lb, MOBA_HEADS * MOBA_HEAD_DIM, D_MODEL), MOBA_HEADS * MOBA_HEAD_DIM, DEEPNORM_BETA),
        "ffn_w_in": nrm(ks[9], (L, D_MODEL, 2 * D_FF), D_MODEL),
        "ffn_w_out": nrm(ks[10], (L, D_FF, D_MODEL), D_FF, DEEPNORM_BETA),
        "ln_mix_g": gain_vec(ks[11], (L, D_MODEL)),
        "ln_mix_b": 0.02 * jax.random.normal(ks[12], (L, D_MODEL), jnp.float32),
        "ln_ffn_g": gain_vec(ks[13], (L, D_MODEL)),
        "ln_ffn_b": 0.02 * jax.random.normal(ks[14], (L, D_MODEL), jnp.float32),
    }


def reference(x, mla_w_dqkv, mla_q_norm, mla_w_uq, mla_kv_norm, mla_w_ukv, mla_w_o,
              moba_w_qkv, moba_w_o, ffn_w_in, ffn_w_out, ln_mix_g, ln_mix_b, ln_ffn_g, ln_ffn_b):
    S = x.shape[1]
    cos_mla, sin_mla = rotary_tables(S, MLA_ROPE_DIM)
    cos_moba, sin_moba = rotary_tables(S, MOBA_ROT_DIM)
    for i in range(DEPTH):
        j = i // N_MIXERS
        if i % N_MIXERS == 0:
            h = mla_mixer(x, cos_mla, sin_mla, mla_w_dqkv[j], mla_q_norm[j], mla_w_uq[j],
                          mla_kv_norm[j], mla_w_ukv[j], mla_w_o[j])
        else:
            h = moba_mixer(x, cos_moba, sin_moba, moba_w_qkv[j], moba_w_o[j])
        x = layer_norm(DEEPNORM_ALPHA * x + h, ln_mix_g[i], ln_mix_b[i])
        x = layer_norm(DEEPNORM_ALPHA * x + swiglu(x, ffn_w_in[i], ffn_w_out[i]), ln_ffn_g[i], ln_ffn_b[i])
    return x
```

```python
import os
import numpy as np
import ml_dtypes
DBG = int(os.environ.get('K_DBG', '0'))
from contextlib import ExitStack
import concourse.bass as bass
import concourse.mybir as mybir
from concourse.bass_utils import run_bass_kernel_spmd

F32 = mybir.dt.float32
BF16 = mybir.dt.bfloat16
ALU = mybir.AluOpType
AF = mybir.ActivationFunctionType
AX = mybir.AxisListType

S_FULL = 16384
D = 1024
NCORE = 8
DFF = 2816
ALPHA = float(4.0 ** 0.25)
LN_EPS = 1e-5
RMS_EPS = 1e-6
NEG = -30000.0
THETA = 500000.0


class T:
    __slots__ = ("w", "r", "excl")

    def __init__(self, excl=False):
        self.w = None
        self.r = []
        self.excl = excl


class Prog:
    ENGS = ("pe", "act", "dve", "pool", "sp")
    NDMA = 12

    NEPOCH = {"pe": 14, "act": 6, "dve": 8, "pool": 4, "sp": 1}
    EPOCH_LEN = 30000

    def __init__(self, nc, st):
        self.nc = nc
        self.ops = {e: [] for e in self.ENGS}
        self.count = {e: 0 for e in self.ENGS}
        self.epoch = {e: 0 for e in self.ENGS}
        self.known = {e: {} for e in self.ENGS}
        self.dma_i = {e: 0 for e in self.ENGS}
        self.sems = {}
        for e in self.ENGS:
            for ep in range(self.NEPOCH[e]):
                self.sems[("E", e, ep)] = st.enter_context(nc.semaphore(f"s_{e}{ep}"))
        for e in ("sp", "pool"):
            for i in range(self.NDMA):
                self.sems[("D", e, i)] = st.enter_context(nc.semaphore(f"d_{e}{i}"))

    def _deps(self, eng, reads, writes):
        need = {}

        def add(ev):
            if ev is not None and need.get(ev[0], 0) < ev[1]:
                need[ev[0]] = ev[1]
        for t in reads:
            add(t.w)
            if t.excl:
                for ev in t.r:
                    add(ev)
        for t in writes:
            add(t.w)
            for ev in t.r:
                add(ev)
        waits = []
        kn = self.known[eng]
        for k, v in need.items():
            if eng == "pe" and k[0] == "E" and k[1] == "pe":
                continue
            if kn.get(k, 0) >= v:
                continue
            kn[k] = v
            waits.append((k, v))
        return waits

    def _commit(self, ev, reads, writes):
        for t in reads:
            if len(t.r) > 6:
                best = {}
                for k, v in t.r:
                    if best.get(k, 0) < v:
                        best[k] = v
                t.r = list(best.items())
            t.r.append(ev)
        for t in writes:
            t.w = ev
            t.r = []

    def do(self, eng, method, reads, writes, *args, **kw):
        waits = self._deps(eng, reads, writes)
        if self.count[eng] >= self.EPOCH_LEN:
            self.epoch[eng] += 1
            self.count[eng] = 0
            assert self.epoch[eng] < self.NEPOCH[eng], eng
        self.count[eng] += 1
        key = ("E", eng, self.epoch[eng])
        ev = (key, self.count[eng])
        self._commit(ev, reads, writes)
        self.ops[eng].append((method, args, kw, waits, (key, 1)))
        return ev

    def dma(self, q, out, in_, reads=(), writes=()):
        i = self.dma_i[q]
        self.dma_i[q] += 1
        slot = ("D", q, i % self.NDMA)
        val = 16 * (i // self.NDMA + 1)
        waits = self._deps(q, reads, writes)
        if i >= self.NDMA and self.known[q].get(slot, 0) < val - 16:
            self.known[q][slot] = val - 16
            waits.append((slot, val - 16))
        ev = (slot, val)
        self._commit(ev, reads, writes)
        self.ops[q].append(("dma_start", (), dict(out=out, in_=in_), waits, (slot, 16)))
        return ev

    def barrier(self):
        evs = []
        for e in self.ENGS:
            if self.count[e]:
                evs.append((("E", e, self.epoch[e]), self.count[e]))
        for q in ("sp", "pool"):
            n = self.dma_i[q]
            for s in range(min(n, self.NDMA)):
                cnt = (n - s + self.NDMA - 1) // self.NDMA
                evs.append((("D", q, s), 16 * cnt))
        for e in self.ENGS:
            waits = []
            for k, v in evs:
                if self.known[e].get(k, 0) < v:
                    self.known[e][k] = v
                    waits.append((k, v))
            if waits:
                self.ops[e].append((None, (), {}, waits, None))

    def emit(self):
        nc = self.nc
        with nc.Block() as block:
            def run(ename):
                def body(eng):
                    for method, args, kw, waits, inc in self.ops[ename]:
                        for k, v in waits:
                            eng.wait_ge(self.sems[k], v)
                        if method is None:
                            continue
                        ins = getattr(eng, method)(*args, **kw)
                        ins.then_inc(self.sems[inc[0]], inc[1])
                return body
            block.tensor(run("pe"))
            block.scalar(run("act"))
            block.vector(run("dve"))
            block.gpsimd(run("pool"))
            block.sync(run("sp"))


def _new(name):
    return bass.Bass("TRN2", target_bir_lowering=False)


def _din(nc, name, shape, dt=F32):
    return nc.dram_tensor(name, list(shape), dt, kind="ExternalInput").ap()


def _dout(nc, name, shape, dt=F32):
    return nc.dram_tensor(name, list(shape), dt, kind="ExternalOutput").ap()


class _Alloc:
    def __init__(self, nc, st):
        self.nc, self.st, self.n = nc, st, 0

    def sb(self, shape, dt):
        self.n += 1
        return self.st.enter_context(self.nc.sbuf_tensor(f"sb{self.n}", list(shape), dt))

    def ps(self, shape, dt=F32):
        self.n += 1
        return self.st.enter_context(self.nc.psum_tensor(f"ps{self.n}", list(shape), dt))


_UID = [0]


class _Alloc2(_Alloc):
    def sb(self, shape, dt):
        _UID[0] += 1
        return self.st.enter_context(self.nc.sbuf_tensor(f"sb{_UID[0]}", list(shape), dt))

    def ps(self, shape, dt=F32):
        _UID[0] += 1
        return self.st.enter_context(self.nc.psum_tensor(f"ps{_UID[0]}", list(shape), dt))


def _to_featmajor(P, G, src32, t_src, xb, t_xb, dstT, t_dstT, q4):
    P.do("act", "copy", [t_src], [t_xb], out=xb, in_=src32)
    for k in range(8):
        P.do("pe", "transpose", [t_xb, G.t_id], [G.t_ps[6]], out=G.pst[:, k * 128:(k + 1) * 128],
             in_=xb[:, k * 128:(k + 1) * 128], identity=G.idb[:])
    P.do("act", "copy", [G.t_ps[6]], [t_dstT], out=dstT[:, :, q4 * 128:(q4 + 1) * 128],
         in_=G.pst[:, 0:1024].rearrange("p (k t) -> p k t", k=8))


def phase_A(nc, P, G, S, x, wdq, gv, cos_tm, sin_tm, latT):
    ps, t_ps = G.ps, G.t_ps
    NT = S // 128
    with ExitStack() as st:
        A = _Alloc2(nc, st)
        wb = A.sb([128, 8, 704], BF16)
        gvs = A.sb([128, 640], F32)
        ccs = A.sb([128, NT, 64], F32)
        sss = A.sb([128, NT, 64], F32)
        junk = A.sb([128, 512], BF16)
        xt = [A.sb([128, D], F32) for _ in range(2)]
        xb = [A.sb([128, D], BF16) for _ in range(2)]
        xTt = [A.sb([128, 8, 128], BF16) for _ in range(2)]
        latn = [A.sb([128, 704], BF16) for _ in range(2)]
        st8 = [A.sb([128, 8], F32) for _ in range(2)]
        t1 = [A.sb([128, 64], F32) for _ in range(2)]
        t2 = [A.sb([128, 64], F32) for _ in range(2)]
        lT = [A.sb([128, 6, 512], BF16) for _ in range(2)]
        t_w, t_g, t_cc, t_ss, t_junk = T(), T(), T(), T(), T()
        t_xt, t_xb, t_xTt, t_latn, t_st, t_t1, t_t2, t_lT = ([T(), T()] for _ in range(8))
        pst5 = ps[5][:].bitcast(BF16)

        P.dma("pool", wb[:], wdq.rearrange("(k p) n -> p k n", p=128), writes=[t_w])
        P.dma("sp", gvs[:], gv, writes=[t_g])
        cv = cos_tm.rearrange("(t p) f -> p t f", p=128)
        sv = sin_tm.rearrange("(t p) f -> p t f", p=128)
        for c0 in range(0, NT, 8):
            c1 = min(NT, c0 + 8)
            P.dma("sp", ccs[:, c0:c1, 0:32], cv[:, c0:c1, :], writes=[t_cc])
            P.dma("sp", ccs[:, c0:c1, 32:64], cv[:, c0:c1, :], writes=[t_cc])
            P.dma("sp", sss[:, c0:c1, 0:32], sv[:, c0:c1, :], writes=[t_ss])
            P.dma("sp", sss[:, c0:c1, 32:64], sv[:, c0:c1, :], writes=[t_ss])
        P.do("act", "mul", [t_ss], [t_ss], out=sss[:, :, 0:32], in_=sss[:, :, 0:32], mul=-1.0)

        for t in range(NT):
            b = t % 2
            pq, pkv = ps[2 * b], ps[2 * b + 1]
            tq, tkv = t_ps[2 * b], t_ps[2 * b + 1]
            P.dma("sp", xt[b][:], x[t * 128:(t + 1) * 128, :], writes=[t_xt[b]])
            _to_featmajor(P, G, xt[b][:], t_xt[b], xb[b][:], t_xb[b], xTt[b], t_xTt[b], 0)
            for k in range(8):
                P.do("pe", "matmul", [t_xTt[b], t_w], [tq], pq[:, 0:384], xTt[b][:, k, :], wb[:, k, 0:384],
                     start=(k == 0), stop=(k == 7))
            for k in range(8):
                P.do("pe", "matmul", [t_xTt[b], t_w], [tkv], pkv[:, 0:320], xTt[b][:, k, :], wb[:, k, 384:704],
                     start=(k == 0), stop=(k == 7))
            s8, ts8 = st8[b], t_st[b]
            P.do("act", "activation", [tq], [t_junk, ts8], out=junk[:, 0:384], in_=pq[:, 0:384], func=AF.Square,
                 accum_out=s8[:, 0:1])
            P.do("act", "activation", [tkv], [t_junk, ts8], out=junk[:, 0:256], in_=pkv[:, 0:256], func=AF.Square,
                 accum_out=s8[:, 1:2])
            P.do("act", "activation", [ts8], [ts8], out=s8[:, 2:3], in_=s8[:, 0:1], func=AF.Sqrt, bias=RMS_EPS,
                 scale=1.0 / 384)
            P.do("act", "activation", [ts8], [ts8], out=s8[:, 3:4], in_=s8[:, 1:2], func=AF.Sqrt, bias=RMS_EPS,
                 scale=1.0 / 256)
            P.do("dve", "reciprocal", [ts8], [ts8], out=s8[:, 4:6], in_=s8[:, 2:4])
            ln, tln = latn[b], t_latn[b]
            P.do("dve", "scalar_tensor_tensor", [tq, ts8, t_g], [tln], out=ln[:, 0:384], in0=pq[:, 0:384],
                 scalar=s8[:, 4:5], in1=gvs[:, 0:384], op0=ALU.mult, op1=ALU.mult)
            P.do("dve", "scalar_tensor_tensor", [tkv, ts8, t_g], [tln], out=ln[:, 384:640], in0=pkv[:, 0:256],
                 scalar=s8[:, 5:6], in1=gvs[:, 384:640], op0=ALU.mult, op1=ALU.mult)
            P.do("dve", "tensor_tensor", [tkv, t_cc], [t_t1[b]], out=t1[b][:], in0=pkv[:, 256:320], in1=ccs[:, t, :],
                 op=ALU.mult)
            P.do("dve", "tensor_tensor", [tkv, t_ss], [t_t2[b]], out=t2[b][:, 0:32], in0=pkv[:, 288:320],
                 in1=sss[:, t, 0:32], op=ALU.mult)
            P.do("dve", "tensor_tensor", [tkv, t_ss], [t_t2[b]], out=t2[b][:, 32:64], in0=pkv[:, 256:288],
                 in1=sss[:, t, 32:64], op=ALU.mult)
            P.do("dve", "tensor_tensor", [t_t1[b], t_t2[b]], [tln], out=ln[:, 640:704], in0=t1[b][:], in1=t2[b][:],
                 op=ALU.add)
            for c in range(6):
                w = 128 if c < 5 else 64
                P.do("pe", "transpose", [tln, G.t_id], [t_ps[5]], out=pst5[0:w, c * 128:(c + 1) * 128],
                     in_=ln[:, c * 128:c * 128 + w], identity=G.idb[:])
            g4, q4 = (t // 4) % 2, t % 4
            P.do("act", "copy", [t_ps[5]], [t_lT[g4]], out=lT[g4][:, 0:5, q4 * 128:(q4 + 1) * 128],
                 in_=pst5[:, 0:640].rearrange("p (c t) -> p c t", c=5))
            P.do("act", "copy", [t_ps[5]], [t_lT[g4]], out=lT[g4][0:64, 5, q4 * 128:(q4 + 1) * 128],
                 in_=pst5[0:64, 640:768])
            if q4 == 3:
                tok = slice((t - 3) * 128, (t + 1) * 128)
                P.dma("sp", latT[0:640, tok].rearrange("(c p) t -> p c t", p=128), lT[g4][:, 0:5, :], reads=[t_lT[g4]])
                P.dma("sp", latT[640:704, tok], lT[g4][0:64, 5, :], reads=[t_lT[g4]])
        P.barrier()


def phase_H(nc, P, G, S, mla, src, W, cosT, sinT, E_d, tri_d, aT):
    ps, t_ps = G.ps, G.t_ps
    NQT = S // 512
    NB = S // 256
    RD = 64 if mla else 32
    HR = RD // 2
    scale = float(192 ** -0.5) if mla else float(128 ** -0.5)
    with ExitStack() as st:
        A = _Alloc2(nc, st)
        kTa = A.sb([128, S], BF16)
        kTb = A.sb([64, S], BF16)
        vaug = A.sb([128, S // 128, 132], BF16)
        qa = [A.sb([128, 512], BF16) for _ in range(2)]
        qb = [A.sb([64, 512], BF16) for _ in range(2)]
        pT = [A.sb([128, 512], BF16) for _ in range(4)]
        trib = A.sb([128, 128], BF16)
        cos_s = [A.sb([RD, 512], F32) for _ in range(2)]
        sin_s = [A.sb([RD, 512], F32) for _ in range(2)]
        t1 = [A.sb([64, 512], F32) for _ in range(2)]
        t2 = [A.sb([64, 512], F32) for _ in range(2)]
        on = A.sb([128, 4, 128], BF16)
        oT = [A.sb([128, 512], BF16) for _ in range(2)]
        rc = A.sb([128, 4], F32)
        if mla:
            wuq = A.sb([128, 3, 192], BF16)
            wrot = A.sb([128, 3, 64], BF16)
            wukv = A.sb([128, 2, 256], BF16)
            cq = [A.sb([128, 3, 512], BF16) for _ in range(2)]
            ckv = [A.sb([128, 2, 512], BF16) for _ in range(2)]
        else:
            wq = A.sb([128, 8, 128], BF16)
            wk = A.sb([128, 8, 128], BF16)
            wv = A.sb([128, 8, 128], BF16)
            wqr = A.sb([128, 8, 32], BF16)
            wkr = A.sb([128, 8, 32], BF16)
            xp = [A.sb([128, 8, 512], BF16) for _ in range(2)]
            kf = A.sb([128, 512], F32)
            ks = A.sb([128, 2], F32)
            kmT = A.sb([128, 64], BF16)
            gm = A.sb([128, 4, 64], F32)
            top = A.sb([128, 32], F32)
            nm = A.sb([128, 4, 64], BF16)
        pjA, pjB = ps[6], ps[7]
        pjB16 = pjB[:].bitcast(BF16)

        t_k = [T() for _ in range(NQT)]
        t_kb = [T() for _ in range(NQT)]
        t_v = [T() for _ in range(NQT)]
        t_ones, t_tri, t_w = T(), T(), T()
        t_qa, t_qb, t_cs, t_t1, t_t2, t_oT, t_src = ([T(), T()] for _ in range(7))
        t_pT = [T() for _ in range(4)]
        t_on = [T() for _ in range(4)]
        t_rc = [T() for _ in range(4)]
        t_kf, t_ks, t_km, t_gm, t_top, t_nm, t_E = T(), T(), T(), T(), T(), T(), T()

        P.dma("pool", trib[:], tri_d, writes=[t_tri])
        P.do("pool", "memset", [], [t_ones], vaug[:, :, 128:129], 1.0)
        if not mla:
            CH = min(S, 2048)
            for c in range(S // CH):
                P.dma("pool", kTb[0:NB, c * CH:(c + 1) * CH], E_d[0:NB, c * CH:(c + 1) * CH], writes=[t_E])

        def rope_rows(pa, pb, tpa, tpb, dst, tdst, b, nrows):
            P.do("dve", "tensor_tensor", [tpa, t_cs[b]], [t_t1[b]], out=t1[b][0:nrows, :], in0=pa[0:nrows, :],
                 in1=cos_s[b][:], op=ALU.mult)
            P.do("dve", "tensor_tensor", [tpb, t_cs[b]], [t_t2[b]], out=t2[b][0:nrows, :], in0=pb[0:nrows, :],
                 in1=sin_s[b][:], op=ALU.mult)
            P.do("dve", "tensor_tensor", [t_t1[b], t_t2[b]], [tdst], out=dst, in0=t1[b][0:nrows, :],
                 in1=t2[b][0:nrows, :], op=ALU.add)

        KB = 64 if mla else NB

        for h in range(8):
            if mla:
                wuq_d, wukv_d = W
                c0 = h * 192
                P.dma("pool", wuq[:], wuq_d[:, c0:c0 + 192].rearrange("(k p) n -> p k n", p=128), writes=[t_w])
                P.dma("pool", wrot[:, :, 0:32], wuq_d[:, c0 + 160:c0 + 192].rearrange("(k p) n -> p k n", p=128), writes=[t_w])
                P.dma("pool", wrot[:, :, 32:64], wuq_d[:, c0 + 128:c0 + 160].rearrange("(k p) n -> p k n", p=128), writes=[t_w])
                P.dma("pool", wukv[:], wukv_d[:, h * 256:(h + 1) * 256].rearrange("(k p) n -> p k n", p=128), writes=[t_w])
                P.do("act", "mul", [t_w], [t_w], out=wrot[:, :, 0:32], in_=wrot[:, :, 0:32], mul=-1.0)
            else:
                wqkv_d = W
                for j, wsb in enumerate((wq, wk, wv)):
                    c0 = j * 1024 + h * 128
                    P.dma("pool", wsb[:], wqkv_d[:, c0:c0 + 128].rearrange("(k p) n -> p k n", p=128), writes=[t_w])
                for j, wsb in enumerate((wqr, wkr)):
                    c0 = j * 1024 + h * 128
                    if DBG & 2:
                        P.do("dve", "memset", [], [t_w], wsb[:], 0.0)
                        continue
                    P.dma("pool", wsb[:, :, 0:16], wqkv_d[:, c0 + 16:c0 + 32].rearrange("(k p) n -> p k n", p=128), writes=[t_w])
                    P.dma("pool", wsb[:, :, 16:32], wqkv_d[:, c0:c0 + 16].rearrange("(k p) n -> p k n", p=128), writes=[t_w])
                P.do("act", "mul", [t_w], [t_w], out=wqr[:, :, 0:16], in_=wqr[:, :, 0:16], mul=-1.0)
                P.do("act", "mul", [t_w], [t_w], out=wkr[:, :, 0:16], in_=wkr[:, :, 0:16], mul=-1.0)
                P.do("dve", "memset", [], [t_km], kmT[:], 0.0)

            def make_proj(i):
                b = i % 2
                tok = slice(i * 512, (i + 1) * 512)
                parts = []
                tA, tB = t_ps[6], t_ps[7]

                def load_tabs():
                    P.dma("sp", cos_s[b][0:HR, :], cosT[:, tok], writes=[t_cs[b]])
                    P.dma("sp", cos_s[b][HR:RD, :], cosT[:, tok], writes=[t_cs[b]])
                    P.dma("sp", sin_s[b][0:HR, :], sinT[:, tok], writes=[t_cs[b]])
                    P.dma("sp", sin_s[b][HR:RD, :], sinT[:, tok], writes=[t_cs[b]])
                if mla:
                    def p0():
                        P.dma("sp", cq[b][:], src[0:384, tok].rearrange("(k p) t -> p k t", p=128), writes=[t_src[b]])
                        P.dma("sp", ckv[b][:], src[384:640, tok].rearrange("(k p) t -> p k t", p=128), writes=[t_src[b]])
                        P.dma("sp", kTb[:, tok], src[640:704, tok], writes=[t_kb[i]])
                        load_tabs()
                        for k in range(3):
                            P.do("pe", "matmul", [t_src[b], t_w], [tA], pjA[:, :], wuq[:, k, 0:128], cq[b][:, k, :],
                                 start=(k == 0), stop=(k == 2))
                        P.do("act", "copy", [tA], [t_qa[b]], out=qa[b][:], in_=pjA[:, :])
                    parts.append(p0)

                    def p1():
                        for k in range(3):
                            P.do("pe", "matmul", [t_src[b], t_w], [tA], pjA[0:64, :], wuq[:, k, 128:192], cq[b][:, k, :],
                                 start=(k == 0), stop=(k == 2))
                        for k in range(3):
                            P.do("pe", "matmul", [t_src[b], t_w], [tB], pjB[0:64, :], wrot[:, k, :], cq[b][:, k, :],
                                 start=(k == 0), stop=(k == 2))
                        rope_rows(pjA, pjB, tA, tB, qb[b][:], t_qb[b], b, 64)
                    parts.append(p1)

                    def p2():
                        for k in range(2):
                            P.do("pe", "matmul", [t_src[b], t_w], [tA], pjA[:, :], wukv[:, k, 0:128], ckv[b][:, k, :],
                                 start=(k == 0), stop=(k == 1))
                        P.do("act", "copy", [tA], [t_k[i]], out=kTa[:, tok], in_=pjA[:, :])
                    parts.append(p2)

                    def p3():
                        for sub in range(4):
                            for k in range(2):
                                P.do("pe", "matmul", [t_src[b], t_w], [tB], pjB[:, sub * 128:(sub + 1) * 128],
                                     ckv[b][:, k, sub * 128:(sub + 1) * 128], wukv[:, k, 128:256],
                                     start=(k == 0), stop=(k == 1))
                        P.do("act", "copy", [tB], [t_v[i]], out=vaug[:, 4 * i:4 * i + 4, 0:128],
                             in_=pjB[:, :].rearrange("p (s d) -> p s d", s=4))
                    parts.append(p3)
                else:
                    def p0():
                        P.dma("sp", xp[b][:], src[:, tok].rearrange("(k p) t -> p k t", p=128), writes=[t_src[b]])
                        load_tabs()
                        for k in range(8):
                            P.do("pe", "matmul", [t_src[b], t_w], [tA], pjA[:, :], wq[:, k, :], xp[b][:, k, :],
                                 start=(k == 0), stop=(k == 7))
                        for k in range(8):
                            P.do("pe", "matmul", [t_src[b], t_w], [tB], pjB[0:32, :], wqr[:, k, :], xp[b][:, k, :],
                                 start=(k == 0), stop=(k == 7))
                        P.do("act", "copy", [tA], [t_qa[b]], out=qa[b][:, :], in_=pjA[:, :])
                        rope_rows(pjA, pjB, tA, tB, qa[b][0:32, :], t_qa[b], b, 32)
                    parts.append(p0)

                    def p1():
                        for k in range(8):
                            P.do("pe", "matmul", [t_src[b], t_w], [tA], pjA[:, :], wk[:, k, :], xp[b][:, k, :],
                                 start=(k == 0), stop=(k == 7))
                        for k in range(8):
                            P.do("pe", "matmul", [t_src[b], t_w], [tB], pjB[0:32, :], wkr[:, k, :], xp[b][:, k, :],
                                 start=(k == 0), stop=(k == 7))
                        P.do("act", "copy", [tA], [t_kf], out=kf[:, :], in_=pjA[:, :])
                        rope_rows(pjA, pjB, tA, tB, kf[0:32, :], t_kf, b, 32)
                        P.do("act", "copy", [t_kf], [t_k[i]], out=kTa[:, tok], in_=kf[:])
                        P.do("dve", "tensor_reduce", [t_kf], [t_ks], out=ks[:], in_=kf[:].rearrange("p (n k) -> p n k", n=2),
                             axis=AX.X, op=ALU.add)
                        P.do("dve", "tensor_scalar", [t_ks], [t_km], out=kmT[:, 2 * i:2 * i + 2], in0=ks[:],
                             scalar1=1.0 / 256, scalar2=None, op0=ALU.mult)
                    parts.append(p1)

                    def p2():
                        for sub in range(4):
                            for k in range(8):
                                P.do("pe", "matmul", [t_src[b], t_w], [tA], pjA[:, sub * 128:(sub + 1) * 128],
                                     xp[b][:, k, sub * 128:(sub + 1) * 128], wv[:, k, :], start=(k == 0), stop=(k == 7))
                        P.do("act", "copy", [tA], [t_v[i]], out=vaug[:, 4 * i:4 * i + 4, 0:128],
                             in_=pjA[:, :].rearrange("p (s d) -> p s d", s=4))
                    parts.append(p2)

                    def p3():
                        if DBG & 1:
                            P.do("dve", "memset", [], [t_qb[b]], qb[b][:], 0.0)
                            return
                        for sub in range(4):
                            P.do("pe", "matmul", [t_qa[b], t_km], [tB], pjB[:, sub * 64:(sub + 1) * 64],
                                 qa[b][:, sub * 128:(sub + 1) * 128], kmT[:], start=True, stop=True)
                        P.do("dve", "tensor_copy", [tB], [t_gm], out=gm[:],
                             in_=pjB[:, 0:256].rearrange("p (s n) -> p s n", s=4))
                        for sub in range(4):
                            own = 2 * i + sub // 2
                            P.do("dve", "memset", [], [t_gm], gm[:, sub, own:64], -1e30)
                        for sub in range(4):
                            P.do("dve", "max", [t_gm], [t_top], out=top[:, sub * 8:(sub + 1) * 8], in_=gm[:, sub, :])
                        for sub in range(4):
                            own = 2 * i + sub // 2
                            P.do("dve", "tensor_scalar", [t_gm, t_top], [t_nm], out=nm[:, sub, :], in0=gm[:, sub, :],
                                 scalar1=top[:, sub * 8 + 2:sub * 8 + 3], scalar2=None, op0=ALU.is_lt)
                            P.do("dve", "tensor_scalar", [t_nm], [t_nm], out=nm[:, sub, :], in0=nm[:, sub, :],
                                 scalar1=NEG, scalar2=None, op0=ALU.mult)
                            P.do("dve", "memset", [], [t_nm], nm[:, sub, own:own + 1], 0.0)
                            if own + 1 < 64:
                                P.do("dve", "memset", [], [t_nm], nm[:, sub, own + 1:64], NEG)
                        for sub in range(4):
                            P.do("pe", "transpose", [t_nm, G.t_id], [tB], out=pjB16[0:64, sub * 128:(sub + 1) * 128],
                                 in_=nm[:, sub, :], identity=G.idb[:])
                        P.do("act", "copy", [tB], [t_qb[b]], out=qb[b][:], in_=pjB16[0:64, 0:512])
                    parts.append(p3)
                return parts

            steps = [(i, kt) for i in range(NQT) for kt in range(4 * i + 4)]

            def emit_qk(si):
                i, kt = steps[si]
                b = si % 2
                c0 = max(0, kt - 4 * i) * 128
                kc = slice(kt * 128, (kt + 1) * 128)
                kbt = t_E if not mla else t_kb[kt // 4]
                P.do("pe", "matmul", [t_k[kt // 4], t_qa[i % 2]], [t_ps[b]], ps[b][:, c0:512], kTa[:, kc],
                     qa[i % 2][:, c0:512], start=True, stop=False)
                P.do("pe", "matmul", [kbt, t_qb[i % 2]], [t_ps[b]], ps[b][:, c0:512], kTb[0:KB, kc],
                     qb[i % 2][0:KB, c0:512], start=False, stop=True)

            def emit_rest(si):
                i, kt = steps[si]
                b = si % 2
                r = si % 4
                j = kt - 4 * i
                c0 = max(0, j) * 128
                P.do("act", "activation", [t_ps[b]], [t_pT[r]], out=pT[r][:, c0:512], in_=ps[b][:, c0:512],
                     func=AF.Exp, scale=scale)
                if j >= 0:
                    P.do("dve", "tensor_tensor", [t_pT[r], t_tri], [t_pT[r]], out=pT[r][:, c0:c0 + 128],
                         in0=pT[r][:, c0:c0 + 128], in1=trib[:], op=ALU.mult)
                for qs in range(max(j, 0), 4):
                    P.do("pe", "matmul", [t_pT[r], t_v[kt // 4], t_ones], [t_ps[2 + qs]], ps[2 + qs][:, 0:129],
                         pT[r][:, qs * 128:(qs + 1) * 128], vaug[:, kt, 0:129], start=(kt == 0), stop=(kt == 4 * i + qs))
                if j >= 0:
                    qs = j
                    P.do("dve", "reciprocal", [t_ps[2 + qs]], [t_rc[qs]], out=rc[:, qs:qs + 1],
                         in_=ps[2 + qs][:, 128:129])
                    P.do("dve", "tensor_scalar", [t_ps[2 + qs], t_rc[qs]], [t_on[qs]], out=on[:, qs, :],
                         in0=ps[2 + qs][:, 0:128], scalar1=rc[:, qs:qs + 1], scalar2=None, op0=ALU.mult)
                    P.do("pe", "transpose", [t_on[qs], G.t_id], [t_ps[7]], out=pjB16[:, qs * 128:(qs + 1) * 128],
                         in_=on[:, qs, :], identity=G.idb[:])
                    P.do("act", "copy", [t_ps[7]], [t_oT[i % 2]], out=oT[i % 2][:, qs * 128:(qs + 1) * 128],
                         in_=pjB16[:, qs * 128:(qs + 1) * 128])
                    if qs == 3:
                        P.dma("sp", aT[h * 128:(h + 1) * 128, i * 512:(i + 1) * 512], oT[i % 2][:], reads=[t_oT[i % 2]])

            for p in make_proj(0):
                p()
            emit_qk(0)
            pending = []
            for si, (i, kt) in enumerate(steps):
                if kt == 0 and i + 1 < NQT:
                    pending = make_proj(i + 1)
                if si + 1 < len(steps):
                    if steps[si + 1][0] != i:
                        for p in pending:
                            p()
                        pending = []
                    emit_qk(si + 1)
                emit_rest(si)
                if pending:
                    pending.pop(0)()
        P.barrier()


def _layer_norm(P, y, t_y, g, b, t_par, s16, t_s):
    P.do("dve", "bn_stats", [t_y], [t_s], out=s16[:, 0:6], in_=y[:, 0:512])
    P.do("dve", "bn_stats", [t_y], [t_s], out=s16[:, 6:12], in_=y[:, 512:1024])
    P.do("dve", "bn_aggr", [t_s], [t_s], out=s16[:, 12:14], in_=s16[:, 0:12])
    P.do("act", "activation", [t_s], [t_s], out=s16[:, 14:15], in_=s16[:, 13:14], func=AF.Sqrt, bias=LN_EPS, scale=1.0)
    P.do("dve", "reciprocal", [t_s], [t_s], out=s16[:, 15:16], in_=s16[:, 14:15])
    P.do("dve", "tensor_scalar", [t_y, t_s], [t_y], out=y, in0=y, scalar1=s16[:, 12:13], scalar2=s16[:, 15:16],
         op0=ALU.subtract, op1=ALU.mult)
    P.do("pool", "tensor_tensor", [t_y, t_par], [t_y], out=y, in0=y, in1=g, op=ALU.mult)
    P.do("pool", "tensor_tensor", [t_y, t_par], [t_y], out=y, in0=y, in1=b, op=ALU.add)


def phase_B(nc, P, G, S, aT, x, wo_d, win_d, wout_d, lnp, x2, x2T, x1s, x1Ts):
    ps, t_ps = G.ps, G.t_ps
    NT = S // 128
    NG = S // 512
    with ExitStack() as st:
        A = _Alloc2(nc, st)
        win = A.sb([128, 8, 2 * DFF], BF16)
        wout = A.sb([128, 22, D], BF16)
        t_win = [T() for _ in range(8)]
        t_wout, t_ln = T(), T()
        t_x1s = [T() for _ in range(NT)]
        t_x1Ts = [T() for _ in range(NG)]

        with ExitStack() as st1:
            A1 = _Alloc2(nc, st1)
            wo = A1.sb([128, 8, D], BF16)
            lns = A1.sb([128, 2 * D], F32)
            at = A1.sb([128, 8, 512], BF16)
            xt = [A1.sb([128, D], F32) for _ in range(2)]
            y = [A1.sb([128, D], F32) for _ in range(2)]
            xb = [A1.sb([128, D], BF16) for _ in range(2)]
            xT4 = A1.sb([128, 8, 512], BF16)
            s16 = [A1.sb([128, 16], F32) for _ in range(2)]
            t_wo, t_at, t_xT4 = T(), T(), T()
            t_xt, t_y, t_xb, t_s = ([T(), T()] for _ in range(4))
            P.dma("sp", lns[:], lnp[:, 0:2 * D], writes=[t_ln])
            P.dma("pool", wo[:], wo_d.rearrange("(k p) n -> p k n", p=128), writes=[t_wo])
            winv = win_d.rearrange("(k p) n -> p k n", p=128)
            for k in range(8):
                P.dma("pool", win[:, k, :], winv[:, k, :], writes=[t_win[k]])
            woutv = wout_d.rearrange("(k p) n -> p k n", p=128)
            for k0 in range(0, 22, 6):
                k1 = min(22, k0 + 6)
                P.dma("pool", wout[:, k0:k1, :], woutv[:, k0:k1, :], writes=[t_wout])
            for t in range(NT):
                b = t % 2
                q4 = t % 4
                if q4 == 0:
                    tok4 = slice(t * 128, (t + 4) * 128)
                    P.dma("sp", at[:], aT[:, tok4].rearrange("(k p) t -> p k t", p=128), writes=[t_at])
                P.dma("sp", xt[b][:], x[t * 128:(t + 1) * 128, :], writes=[t_xt[b]])
                for hf in range(2):
                    bank = 2 * b + hf
                    for k in range(8):
                        P.do("pe", "matmul", [t_at, t_wo], [t_ps[bank]], ps[bank][:, :],
                             at[:, k, q4 * 128:(q4 + 1) * 128], wo[:, k, hf * 512:(hf + 1) * 512],
                             start=(k == 0), stop=(k == 7))
                    P.do("dve", "scalar_tensor_tensor", [t_xt[b], t_ps[bank]], [t_y[b]],
                         out=y[b][:, hf * 512:(hf + 1) * 512], in0=xt[b][:, hf * 512:(hf + 1) * 512], scalar=ALPHA,
                         in1=ps[bank][:, :], op0=ALU.mult, op1=ALU.add)
                _layer_norm(P, y[b][:], t_y[b], lns[:, 0:D], lns[:, D:2 * D], t_ln, s16[b], t_s[b])
                P.dma("sp", x1s[t * 128:(t + 1) * 128, :], y[b][:], reads=[t_y[b]], writes=[t_x1s[t]])
                _to_featmajor(P, G, y[b][:], t_y[b], xb[b][:], t_xb[b], xT4, t_xT4, q4)
                if q4 == 3:
                    tok4 = slice((t - 3) * 128, (t + 1) * 128)
                    P.dma("sp", x1Ts[:, tok4].rearrange("(k p) t -> p k t", p=128), xT4[:], reads=[t_xT4],
                          writes=[t_x1Ts[t // 4]])
            P.barrier()

        with ExitStack() as st2:
            A2 = _Alloc2(nc, st2)
            xg = A2.sb([128, 8, 512], BF16)
            lns = A2.sb([128, 2 * D], F32)
            actT = A2.sb([128, 22, 512], BF16)
            sg = [A2.sb([128, 512], BF16) for _ in range(2)]
            z = [A2.sb([128, D], F32) for _ in range(2)]
            xb = [A2.sb([128, D], BF16) for _ in range(2)]
            xT4 = A2.sb([128, 8, 512], BF16)
            s16 = [A2.sb([128, 16], F32) for _ in range(2)]
            t_sg, t_z, t_xb, t_s = ([T(), T()] for _ in range(4))
            t_xg, t_xT4 = T(), T()
            t_act = [T() for _ in range(22)]
            P.dma("sp", lns[:], lnp[:, 2 * D:4 * D], writes=[t_ln])
            for g in range(NG):
                tok4 = slice(g * 512, (g + 1) * 512)
                P.dma("sp", xg[:], x1Ts[:, tok4].rearrange("(k p) t -> p k t", p=128), reads=[t_x1Ts[g]],
                      writes=[t_xg])
                for f in range(22):
                    pb = 2 * (f % 2)
                    for k in range(8):
                        P.do("pe", "matmul", [t_xg, t_win[k]], [t_ps[pb]], ps[pb][:, :],
                             win[:, k, f * 128:(f + 1) * 128], xg[:, k, :], start=(k == 0), stop=(k == 7))
                    for k in range(8):
                        P.do("pe", "matmul", [t_xg, t_win[k]], [t_ps[pb + 1]], ps[pb + 1][:, :],
                             win[:, k, DFF + f * 128:DFF + (f + 1) * 128], xg[:, k, :], start=(k == 0), stop=(k == 7))
                    P.do("act", "activation", [t_ps[pb]], [t_sg[f % 2]], out=sg[f % 2][:], in_=ps[pb][:, :], func=AF.Silu)
                    P.do("dve", "tensor_tensor", [t_sg[f % 2], t_ps[pb + 1]], [t_act[f]], out=actT[:, f, :],
                         in0=sg[f % 2][:], in1=ps[pb + 1][:, :], op=ALU.mult)
                for sub in range(4):
                    t = g * 4 + sub
                    b = t % 2
                    P.dma("sp", z[b][:], x1s[t * 128:(t + 1) * 128, :], reads=[t_x1s[t]], writes=[t_z[b]])
                    for hf in range(2):
                        bank = 4 + hf
                        for f in range(22):
                            P.do("pe", "matmul", [t_act[f], t_wout], [t_ps[bank]], ps[bank][:, :],
                                 actT[:, f, sub * 128:(sub + 1) * 128], wout[:, f, hf * 512:(hf + 1) * 512],
                                 start=(f == 0), stop=(f == 21))
                        P.do("dve", "scalar_tensor_tensor", [t_z[b], t_ps[bank]], [t_z[b]],
                             out=z[b][:, hf * 512:(hf + 1) * 512], in0=z[b][:, hf * 512:(hf + 1) * 512], scalar=ALPHA,
                             in1=ps[bank][:, :], op0=ALU.mult, op1=ALU.add)
                    _layer_norm(P, z[b][:], t_z[b], lns[:, 0:D], lns[:, D:2 * D], t_ln, s16[b], t_s[b])
                    P.dma("sp", x2[t * 128:(t + 1) * 128, :], z[b][:], reads=[t_z[b]])
                    if x2T is not None:
                        _to_featmajor(P, G, z[b][:], t_z[b], xb[b][:], t_xb[b], xT4, t_xT4, sub)
                        if sub == 3:
                            P.dma("sp", x2T[:, tok4].rearrange("(k p) t -> p k t", p=128), xT4[:], reads=[t_xT4])
            P.barrier()


class _G:
    pass


def build_all(S, upto=5):
    nc = _new("all")
    names = []

    def I(name, shape):
        names.append(name)
        return _din(nc, name, shape)
    x = I("x", [S, D])
    wdq = I("wdq", [D, 704])
    gv = I("gv", [128, 640])
    cos_tm = I("cos_tm", [S, 32])
    sin_tm = I("sin_tm", [S, 32])
    ident = I("ident", [128, 128])
    if upto >= 2:
        wuq = I("wuq", [384, 1536])
        wukv = I("wukv", [256, 2048])
        cosT_m = I("cosT_m", [32, S])
        sinT_m = I("sinT_m", [32, S])
        tri_d = I("tri", [128, 128])
        E_d = I("E", [64, S])
    if upto >= 3:
        wo0 = I("wo0", [D, D])
        win0 = I("win0", [D, 2 * DFF])
        wout0 = I("wout0", [DFF, D])
        lnp0 = I("lnp0", [128, 4 * D])
    if upto >= 4:
        wqkv = I("wqkv", [D, 3072])
        cosT_b = I("cosT_b", [16, S])
        sinT_b = I("sinT_b", [16, S])
    if upto >= 5:
        wo1 = I("wo1", [D, D])
        win1 = I("win1", [D, 2 * DFF])
        wout1 = I("wout1", [DFF, D])
        lnp1 = I("lnp1", [128, 4 * D])
    nc._in_names = names
    out = _dout(nc, "out", [S, D])
    latT = nc.dram_tensor("latT", [704, S], BF16).ap()
    aT = nc.dram_tensor("aT", [D, S], BF16).ap()
    xm = nc.dram_tensor("xm", [S, D], F32).ap()
    xmT = nc.dram_tensor("xmT", [D, S], BF16).ap()
    x1s = nc.dram_tensor("x1s", [S, D], F32).ap()
    x1Ts = nc.dram_tensor("x1Ts", [D, S], BF16).ap()
    with ExitStack() as st:
        P = Prog(nc, st)
        A = _Alloc2(nc, st)
        G = _G()
        G.idb = A.sb([128, 128], BF16)
        G.ps = [A.ps([128, 512]) for _ in range(8)]
        G.pst = G.ps[6][:].bitcast(BF16)
        G.t_ps = [T(excl=True) for _ in range(8)]
        G.t_id = T()
        P.dma("pool", G.idb[:], ident, writes=[G.t_id])
        phase_A(nc, P, G, S, x, wdq, gv, cos_tm, sin_tm, latT)
        if upto >= 2:
            phase_H(nc, P, G, S, True, latT, (wuq, wukv), cosT_m, sinT_m, E_d, tri_d, aT)
        if upto >= 3:
            phase_B(nc, P, G, S, aT, x, wo0, win0, wout0, lnp0, xm, xmT, x1s, x1Ts)
        if upto >= 4:
            phase_H(nc, P, G, S, False, xmT, wqkv, cosT_b, sinT_b, E_d, tri_d, aT)
        if upto >= 5:
            phase_B(nc, P, G, S, aT, xm, wo1, win1, wout1, lnp1, out, None, x1s, x1Ts)
        P.emit()
    return nc


_CACHE = {}


def _rot_tables(S, rot_dim):
    inv = (np.float32(THETA) ** (-np.arange(0, rot_dim, 2, dtype=np.float32) / np.float32(rot_dim))).astype(np.float32)
    ang = (np.arange(S, dtype=np.float32)[:, None] * inv[None, :]).astype(np.float32)
    return np.cos(ang).astype(np.float32), np.sin(ang).astype(np.float32)


def make_inputs(S, x, mla_w_dqkv, mla_q_norm, mla_w_uq, mla_kv_norm, mla_w_ukv, mla_w_o, moba_w_qkv, moba_w_o,
                ffn_w_in, ffn_w_out, ln_mix_g, ln_mix_b, ln_ffn_g, ln_ffn_b):
    f32 = np.float32
    c = np.ascontiguousarray
    cos_m, sin_m = _rot_tables(S, 64)
    cos_b, sin_b = _rot_tables(S, 32)

    def lnp(i):
        v = np.concatenate([ln_mix_g[i], ln_mix_b[i], ln_ffn_g[i], ln_ffn_b[i]])
        return c(np.broadcast_to(v[None, :], (128, 4 * D))).astype(f32)
    gv = c(np.broadcast_to(np.concatenate([mla_q_norm[0], mla_kv_norm[0]])[None, :], (128, 640))).astype(f32)
    E = np.zeros((64, S), f32)
    for n in range(S // 256):
        E[n, n * 256:(n + 1) * 256] = 1.0
    return dict(
        x=c(np.asarray(x, f32).reshape(S, D)), wdq=c(mla_w_dqkv[0]), gv=gv, wuq=c(mla_w_uq[0]), wukv=c(mla_w_ukv[0]),
        wo0=c(mla_w_o[0]), wqkv=c(moba_w_qkv[0]), wo1=c(moba_w_o[0]), win0=c(ffn_w_in[0]), win1=c(ffn_w_in[1]),
        wout0=c(ffn_w_out[0]), wout1=c(ffn_w_out[1]), lnp0=lnp(0), lnp1=lnp(1),
        cos_tm=cos_m, sin_tm=sin_m, cosT_m=c(cos_m.T), sinT_m=c(sin_m.T), cosT_b=c(cos_b.T), sinT_b=c(sin_b.T),
        E=E, tri=np.triu(np.ones((128, 128), f32)), ident=np.eye(128, dtype=f32))


def run_model(S, **inputs):
    if S not in _CACHE:
        _CACHE[S] = build_all(S)
    m = make_inputs(S, **inputs)
    m = {k: m[k] for k in _CACHE[S]._in_names}
    res = run_bass_kernel_spmd(_CACHE[S], [m], core_ids=[0])
    return res.results[0]["out"]


def run_upto(S, upto, **inputs):
    nc = build_all(S, upto)
    m = make_inputs(S, **inputs)
    m = {k: m[k] for k in nc._in_names}
    res = run_bass_kernel_spmd(nc, [m], core_ids=[0])
    return res.results[0]["out"]


def kernel(**inputs):
    out = run_model(S_FULL, **inputs)
    return np.asarray(out, np.float32).reshape(1, S_FULL, D)
```
